# Optimizing a Trainium2 kernel written in Bass

```python
import jax, jax.numpy as jnp
from jax import lax
import numpy as np

D_MODEL = 2048
BATCH = 2
SEQ = 4096
DEPTH = 4

GRID_W = 64
CTX_LEN = 256
NA_HEADS = 8
NA_HEAD_DIM = 128
NA_WIDTH = NA_HEADS * NA_HEAD_DIM
WIN_R = 8
WIN_C = 16
ROPE_BASE = 10000.0
SSD_HEADS = 16
SSD_HEAD_DIM = 64
SSD_INNER = SSD_HEADS * SSD_HEAD_DIM
SSD_GROUPS = 2
SSD_STATE = 128
SSD_CHUNK = 128
CONV_K = 5
CONV_CH = SSD_INNER + 2 * SSD_GROUPS * SSD_STATE
IN_W = 3 * NA_WIDTH + SSD_INNER + CONV_CH + 2 * SSD_HEADS
MIX_W = NA_WIDTH + SSD_INNER
F_GROUPS = 4
D_FF = 5632
N_MOD = 9
N_EVEN = (DEPTH + 1) // 2
N_ODD = DEPTH // 2
EPS = 1e-6

kernel_name = 'hybrid_natten_ssd_fnet_macaron_dit'


def rms_norm(x, g):
    xf = x.astype(jnp.float32)
    y = xf * lax.rsqrt(jnp.mean(xf * xf, axis=-1, keepdims=True) + EPS)
    return (y * g.astype(jnp.float32)).astype(x.dtype)


def modulate(x, g, shift, scale):
    return rms_norm(x, g) * (1 + scale) + shift


def swiglu(h, w1, w3, w2):
    return (jax.nn.silu(h @ w1) * (h @ w3)) @ w2


def axial_rope(x, row, col):
    quarter = x.shape[-1] // 4
    inv_freq = ROPE_BASE ** (-jnp.arange(quarter, dtype=jnp.float32) / quarter)

    def rotate(xa, pos):
        ang = pos.astype(jnp.float32)[:, None] * inv_freq[None, :]
        cos = jnp.cos(ang)[None, :, None, :]
        sin = jnp.sin(ang)[None, :, None, :]
        x1, x2 = jnp.split(xa.astype(jnp.float32), 2, axis=-1)
        return jnp.concatenate([x1 * cos - x2 * sin, x2 * cos + x1 * sin], axis=-1)

    xr, xc = jnp.split(x, 2, axis=-1)
    return jnp.concatenate([rotate(xr, row), rotate(xc, col)], axis=-1).astype(x.dtype)


def neighbourhood_attention(q, k, v, kc, vc, rpb, row, col):
    bsz, L, H, dh = q.shape
    rows = L // GRID_W
    wr = min(WIN_R, rows)
    scale = dh ** -0.5
    qg = axial_rope(q, row, col).reshape(bsz, rows, GRID_W, H, dh)
    kg = axial_rope(k, row, col).reshape(bsz, rows, GRID_W, H, dh)
    vg = v.reshape(bsz, rows, GRID_W, H, dh)
    r = jnp.arange(rows)
    r0 = jnp.clip(r - wr // 2, 0, rows - wr)
    row_idx = r0[:, None] + jnp.arange(wr)[None, :]
    kb = kg[:, row_idx]
    vb = vg[:, row_idx]
    s_win = jnp.einsum('brqhd,brwkhd->bhrqwk', qg, kb).astype(jnp.float32) * scale
    cq = jnp.arange(GRID_W)
    c0 = jnp.clip(cq - WIN_C // 2, 0, GRID_W - WIN_C)
    col_ok = (cq[None, :] >= c0[:, None]) & (cq[None, :] < c0[:, None] + WIN_C)
    dr = row_idx - r[:, None] + (WIN_R - 1)
    dc = jnp.clip(cq[None, :] - cq[:, None], -(WIN_C - 1), WIN_C - 1) + (WIN_C - 1)
    bias = rpb.astype(jnp.float32)[:, dr[:, None, :, None], dc[None, :, None, :]]
    s_win = jnp.where(col_ok[:, None, :], s_win + bias, -jnp.inf)
    q_plain = q.reshape(bsz, rows, GRID_W, H, dh)
    s_ctx = jnp.einsum('brqhd,bchd->bhrqc', q_plain, kc).astype(jnp.float32) * scale
    n_win = wr * GRID_W
    logits = jnp.concatenate([s_win.reshape(bsz, H, rows, GRID_W, n_win), s_ctx], axis=-1)
    p = jax.nn.softmax(logits, axis=-1).astype(v.dtype)
    p_win = p[..., :n_win].reshape(bsz, H, rows, GRID_W, wr, GRID_W)
    p_ctx = p[..., n_win:]
    o = jnp.einsum('bhrqwk,brwkhd->brqhd', p_win, vb) + jnp.einsum('bhrqc,bchd->brqhd', p_ctx, vc)
    return o.reshape(bsz, L, H * dh)


def context_attention(qc, kc, vc):
    bsz, lc, H, dh = qc.shape
    s = jnp.einsum('bqhd,bkhd->bhqk', qc, kc).astype(jnp.float32) * dh ** -0.5
    p = jax.nn.softmax(s, axis=-1).astype(vc.dtype)
    return jnp.einsum('bhqk,bkhd->bqhd', p, vc).reshape(bsz, lc, H * dh)


def centred_dwconv(x, w, b):
    y = lax.conv_general_dilated(x, w[:, None, :].astype(x.dtype), window_strides=(1,),
                                 padding=[(CONV_K // 2, CONV_K // 2)],
                                 dimension_numbers=('NWC', 'WIO', 'NWC'),
                                 feature_group_count=x.shape[-1])
    return y + b.astype(x.dtype)


def ssd_chunked(x, dt, A, bm, cm, h0, with_y):
    bsz, L, H, P = x.shape
    Q = SSD_CHUNK
    nc = L // Q
    rep = H // SSD_GROUPS
    f32 = jnp.float32
    xq = x.astype(f32).reshape(bsz, nc, Q, H, P)
    dtq = dt.reshape(bsz, nc, Q, H)
    bh = jnp.repeat(bm.astype(f32), rep, axis=2).reshape(bsz, nc, Q, H, SSD_STATE)
    ch = jnp.repeat(cm.astype(f32), rep, axis=2).reshape(bsz, nc, Q, H, SSD_STATE)
    a_cum = jnp.cumsum(dtq * A.astype(f32), axis=2)
    xdt = xq * dtq[..., None]
    a_end = a_cum[:, :, -1]
    states = jnp.einsum('bcjhn,bcjhp->bchpn', bh * jnp.exp(a_end[:, :, None] - a_cum)[..., None], xdt)

    def step(h, inp):
        dec, s = inp
        return dec[:, :, None, None] * h + s, h

    h_fin, h_in = lax.scan(step, h0, (jnp.moveaxis(jnp.exp(a_end), 1, 0), jnp.moveaxis(states, 1, 0)))
    if not with_y:
        return None, h_fin
    h_in = jnp.moveaxis(h_in, 0, 1)
    seg = a_cum[:, :, :, None, :] - a_cum[:, :, None, :, :]
    lower = jnp.tril(jnp.ones((Q, Q), dtype=bool))[None, None, :, :, None]
    decay = jnp.exp(jnp.where(lower, seg, -jnp.inf))
    scores = jnp.einsum('bcihn,bcjhn->bcijh', ch, bh) * decay
    y = (jnp.einsum('bcijh,bcjhp->bcihp', scores, xdt)
         + jnp.einsum('bcihn,bchpn->bcihp', ch * jnp.exp(a_cum)[..., None], h_in))
    return y.reshape(bsz, L, H, P), h_fin


def ssd_bidirectional(xbc, dt_raw, h0_f, h0_b, conv_w, conv_b, dt_bias, a_log, d_skip, with_y):
    xbc = jax.nn.silu(centred_dwconv(xbc, conv_w, conv_b))
    bsz, L, _ = xbc.shape
    gn = SSD_GROUPS * SSD_STATE
    xs = xbc[..., :SSD_INNER].reshape(bsz, L, SSD_HEADS, SSD_HEAD_DIM)
    bm = xbc[..., SSD_INNER:SSD_INNER + gn].reshape(bsz, L, SSD_GROUPS, SSD_STATE)
    cm = xbc[..., SSD_INNER + gn:].reshape(bsz, L, SSD_GROUPS, SSD_STATE)
    A = -jnp.exp(a_log.astype(jnp.float32))
    dt = jax.nn.softplus(dt_raw.astype(jnp.float32).reshape(bsz, L, 2, SSD_HEADS) + dt_bias.astype(jnp.float32))
    flip = lambda t: jnp.flip(t, axis=1)
    y_f, h_f = ssd_chunked(xs, dt[:, :, 0], A[0], bm, cm, h0_f, with_y)
    y_b, h_b = ssd_chunked(flip(xs), flip(dt[:, :, 1]), A[1], flip(bm), flip(cm), h0_b, with_y)
    if not with_y:
        return None, h_f, h_b
    y = y_f + flip(y_b) + d_skip.astype(jnp.float32)[:, None] * xs.astype(jnp.float32)
    return y.reshape(bsz, L, SSD_INNER), h_f, h_b


def gated_rmsnorm(y, z, g):
    yz = y.astype(jnp.float32) * jax.nn.silu(z.astype(jnp.float32))
    bsz, L, _ = yz.shape
    yg = yz.reshape(bsz, L, SSD_GROUPS, SSD_INNER // SSD_GROUPS)
    yg = yg * lax.rsqrt(jnp.mean(yg * yg, axis=-1, keepdims=True) + EPS)
    return (yg.reshape(bsz, L, SSD_INNER) * g.astype(jnp.float32)).astype(z.dtype)


def even_mixer(h, hc, row, col, w_in, w_out, qk_g, rpb, conv_w, conv_b, dt_bias, a_log, d_skip, ssd_g, ctx_out):
    cuts = [NA_WIDTH, 2 * NA_WIDTH, 3 * NA_WIDTH, 3 * NA_WIDTH + SSD_INNER, 3 * NA_WIDTH + SSD_INNER + CONV_CH]
    q, k, v, z, xbc, dt = jnp.split(h @ w_in, cuts, axis=-1)
    qc, kc, vc, zc, xbcc, dtc = jnp.split(hc @ w_in, cuts, axis=-1)
    heads = lambda t: t.reshape(t.shape[0], t.shape[1], NA_HEADS, NA_HEAD_DIM)
    q, k, v = rms_norm(heads(q), qk_g[0]), rms_norm(heads(k), qk_g[1]), heads(v)
    qc, kc, vc = rms_norm(heads(qc), qk_g[0]), rms_norm(heads(kc), qk_g[1]), heads(vc)
    attn = neighbourhood_attention(q, k, v, kc, vc, rpb, row, col)
    h0 = jnp.zeros((hc.shape[0], SSD_HEADS, SSD_HEAD_DIM, SSD_STATE), jnp.float32)
    yc_ssd, hf, hb = ssd_bidirectional(xbcc, dtc, h0, h0, conv_w, conv_b, dt_bias, a_log, d_skip, ctx_out)
    y_ssd, _, _ = ssd_bidirectional(xbc, dt, hf, hb, conv_w, conv_b, dt_bias, a_log, d_skip, True)
    out = jnp.concatenate([attn, gated_rmsnorm(y_ssd, z, ssd_g)], axis=-1) @ w_out
    if not ctx_out:
        return out, None
    attn_c = context_attention(qc, kc, vc)
    out_c = jnp.concatenate([attn_c, gated_rmsnorm(yc_ssd, zc, ssd_g)], axis=-1) @ w_out
    return out, out_c


def fourier_mix(h, w):
    bsz, L, dm = h.shape
    hg = h.astype(jnp.float32).reshape(bsz, L, F_GROUPS, dm // F_GROUPS)
    f = jnp.fft.fft2(hg, axes=(1, 3), norm='ortho').real
    return f.reshape(bsz, L, dm).astype(h.dtype) @ w


def setup_inputs(seed: int = 0) -> dict:
    key = jax.random.key(seed)
    ks = jax.random.split(key, 24)
    nrm = lambda k, shape, s: jax.random.normal(k, shape, jnp.float32) * s
    dt0 = jnp.exp(jax.random.uniform(ks[17], (N_EVEN, 2, SSD_HEADS), jnp.float32)
                  * (np.log(0.1) - np.log(0.001)) + np.log(0.001))
    return {
        'x': nrm(ks[0], (BATCH, SEQ, D_MODEL), 1.0),
        'c': nrm(ks[1], (BATCH, D_MODEL), 1.0),
        'ctx': nrm(ks[2], (BATCH, CTX_LEN, D_MODEL), 1.0),
        'c_ctx': nrm(ks[3], (D_MODEL,), 1.0),
        'mod_w': nrm(ks[4], (DEPTH, D_MODEL, N_MOD * D_MODEL), 0.25 * D_MODEL ** -0.5),
        'mod_b': nrm(ks[5], (DEPTH, N_MOD * D_MODEL), 0.02),
        'norm_g': 1.0 + nrm(ks[6], (DEPTH, 3, D_MODEL), 0.02),
        'ffn_w1': nrm(ks[7], (DEPTH, 2, D_MODEL, D_FF), D_MODEL ** -0.5),
        'ffn_w3': nrm(ks[8], (DEPTH, 2, D_MODEL, D_FF), D_MODEL ** -0.5),
        'ffn_w2': nrm(ks[9], (DEPTH, 2, D_FF, D_MODEL), D_FF ** -0.5),
        'mix_w_in': nrm(ks[10], (N_EVEN, D_MODEL, IN_W), D_MODEL ** -0.5),
        'mix_w_out': nrm(ks[11], (N_EVEN, MIX_W, D_MODEL), MIX_W ** -0.5),
        'qk_g': 1.0 + nrm(ks[12], (N_EVEN, 2, NA_HEAD_DIM), 0.02),
        'rpb': nrm(ks[13], (N_EVEN, NA_HEADS, 2 * WIN_R - 1, 2 * WIN_C - 1), 0.1),
        'conv_w': nrm(ks[14], (N_EVEN, CONV_K, CONV_CH), CONV_K ** -0.5),
        'conv_b': nrm(ks[15], (N_EVEN, CONV_CH), 0.02),
        'dt_bias': dt0 + jnp.log(-jnp.expm1(-dt0)),
        'a_log': jnp.log(jax.random.uniform(ks[18], (N_EVEN, 2, SSD_HEADS), jnp.float32, 1.0, 16.0)),
        'ssd_d': 1.0 + nrm(ks[19], (N_EVEN, SSD_HEADS), 0.1),
        'ssd_norm_g': 1.0 + nrm(ks[20], (N_EVEN, SSD_INNER), 0.02),
        'fourier_w': nrm(ks[21], (N_ODD, D_MODEL, D_MODEL), D_MODEL ** -0.5),
    }


def reference(x, c, ctx, c_ctx, mod_w, mod_b, norm_g, ffn_w1, ffn_w3, ffn_w2, mix_w_in, mix_w_out, qk_g, rpb,
              conv_w, conv_b, dt_bias, a_log, ssd_d, ssd_norm_g, fourier_w):
    L = x.shape[1]
    t = jnp.arange(L)
    row = t // GRID_W
    col = t % GRID_W
    last_ctx = ((DEPTH - 1) // 2) * 2
    xc = ctx
    for i in range(DEPTH):
        use_ctx = i <= last_ctx
        ctx_out = i < last_ctx
        j = i // 2
        m = jnp.split((jax.nn.silu(c) @ mod_w[i] + mod_b[i])[:, None, :], N_MOD, axis=-1)
        x = x + 0.5 * m[2] * swiglu(modulate(x, norm_g[i, 0], m[0], m[1]), ffn_w1[i, 0], ffn_w3[i, 0], ffn_w2[i, 0])
        h = modulate(x, norm_g[i, 1], m[3], m[4])
        hc = None
        mc = None
        if use_ctx:
            mc = jnp.split(jax.nn.silu(c_ctx) @ mod_w[i] + mod_b[i], N_MOD, axis=-1)
            xc = xc + 0.5 * mc[2] * swiglu(modulate(xc, norm_g[i, 0], mc[0], mc[1]), ffn_w1[i, 0], ffn_w3[i, 0], ffn_w2[i, 0])
            hc = modulate(xc, norm_g[i, 1], mc[3], mc[4])
        if i % 2 == 0:
            mix, mix_c = even_mixer(h, hc, row, col, mix_w_in[j], mix_w_out[j], qk_g[j], rpb[j], conv_w[j], conv_b[j],
                                    dt_bias[j], a_log[j], ssd_d[j], ssd_norm_g[j], ctx_out)
        else:
            mix = fourier_mix(h, fourier_w[j])
            mix_c = fourier_mix(hc, fourier_w[j]) if ctx_out else None
        x = x + m[5] * mix
        x = x + 0.5 * m[8] * swiglu(modulate(x, norm_g[i, 2], m[6], m[7]), ffn_w1[i, 1], ffn_w3[i, 1], ffn_w2[i, 1])
        if ctx_out:
            xc = xc + mc[5] * mix_c
            xc = xc + 0.5 * mc[8] * swiglu(modulate(xc, norm_g[i, 2], mc[6], mc[7]), ffn_w1[i, 1], ffn_w3[i, 1], ffn_w2[i, 1])
    return x
```

```python
import numpy as np
import ml_dtypes
from contextlib import ExitStack
import concourse.bass as bass
import concourse.mybir as mybir
from concourse.bass_utils import run_bass_kernel_spmd

bf = ml_dtypes.bfloat16


F32 = mybir.dt.float32
BF16 = mybir.dt.bfloat16
AF = mybir.ActivationFunctionType
ALU = mybir.AluOpType
AX = mybir.AxisListType


class Tok:
    __slots__ = ("k", "v")

    def __init__(self, k, v=None):
        self.k = k
        self.v = v


class Prog:
    NDMA = 12

    def __init__(self, nc, stack):
        self.nc = nc
        self.stack = stack
        self.E = {"pe": nc.tensor, "act": nc.scalar, "dve": nc.vector, "pool": nc.gpsimd, "sp": nc.sync}
        self.sems = {}
        self.cnt = {}
        self.seen = {e: {} for e in self.E}
        self.lastw = {}
        self.readers = {}
        self.pending = {e: [] for e in self.E}
        for e in ("pe", "act", "dve", "pool"):
            self.sems[e] = stack.enter_context(nc.semaphore("s_" + e))
            self.cnt[e] = 0
        self.dq = {}
        for q in ("sp", "pool", "act"):
            ks = []
            for i in range(self.NDMA):
                k = "d_%s%d" % (q, i)
                self.sems[k] = stack.enter_context(nc.semaphore(k))
                self.cnt[k] = 0
                ks.append(k)
            self.dq[q] = [ks, 0]
        self.ninst = 0
        self.out_toks = []

    def sb(self, name, shape, dt, stack=None):
        return (stack or self.stack).enter_context(self.nc.sbuf_tensor(name, list(shape), dt))

    def barrier(self):
        for e in ("pe", "act", "dve", "pool"):
            assert not self.pending[e], "barrier with unresolved tokens on " + e
        for e in self.E:
            for e2 in ("pe", "act", "dve", "pool"):
                if e2 != e and self.cnt[e2] > 0:
                    self._wait(e, Tok(e2, self.cnt[e2]))
            for q in self.dq:
                for k in self.dq[q][0]:
                    if self.cnt[k] > 0:
                        self._wait(e, Tok(k, self.cnt[k]))

    def ps(self, name, shape, dt=F32):
        return self.stack.enter_context(self.nc.psum_tensor(name, list(shape), dt))

    def _wait(self, e, tok):
        if e == "pe" and tok.k == "pe":
            return
        if tok.v is None:
            raise RuntimeError("unresolved token on %s (mark a milestone)" % tok.k)
        if self.seen[e].get(tok.k, 0) >= tok.v:
            return
        self.E[e].wait_ge(self.sems[tok.k], tok.v)
        self.seen[e][tok.k] = tok.v
        self.ninst += 1

    def _deps(self, e, reads, writes):
        for r in reads:
            t = self.lastw.get(r)
            if t is not None:
                self._wait(e, t)
        for w in writes:
            t = self.lastw.get(w)
            if t is not None:
                self._wait(e, t)
            for t in self.readers.get(w, ()):
                self._wait(e, t)

    def _record(self, tok, reads, writes):
        for r in reads:
            self.readers.setdefault(r, []).append(tok)
            if len(self.readers[r]) > 24:
                d = {}
                for t in self.readers[r]:
                    if t.v is None or t.k not in d or d[t.k].v is None or (d[t.k].v < t.v):
                        if t.v is None:
                            d[(t.k, id(t))] = t
                        else:
                            d[t.k] = t
                self.readers[r] = list(d.values())
        for w in writes:
            self.lastw[w] = tok
            self.readers[w] = []

    @staticmethod
    def _psx(reads, writes):
        extra = [r for r in reads if isinstance(r, tuple) and r and r[0] == "ps" and r not in writes]
        return list(writes) + extra if extra else writes

    def op(self, e, fn, reads=(), writes=(), inc=True):
        writes = self._psx(reads, writes)
        self._deps(e, reads, writes)
        ins = fn()
        self.ninst += 1
        if inc:
            self.cnt[e] += 1
            ins.then_inc(self.sems[e], 1)
            tok = Tok(e, self.cnt[e])
            for p in self.pending[e]:
                p.v = self.cnt[e]
            self.pending[e] = []
        else:
            tok = Tok(e, None)
            self.pending[e].append(tok)
        self._record(tok, reads, writes)
        return tok

    def dma(self, q, out, in_, reads=(), writes=(), is_out=False, **kw):
        self._deps(q, reads, writes)
        ks, i = self.dq[q]
        k = ks[i % len(ks)]
        self.dq[q][1] = i + 1
        if self.cnt[k] > 0:
            self._wait(q, Tok(k, self.cnt[k]))
        ins = self.E[q].dma_start(out=out, in_=in_, **kw)
        self.cnt[k] += 16
        ins.then_inc(self.sems[k], 16)
        self.ninst += 1
        tok = Tok(k, self.cnt[k])
        self._record(tok, reads, writes)
        if is_out:
            self.out_toks.append(tok)
        return tok

    def finish(self):
        for t in self.out_toks:
            self._wait("sp", t)
        self.out_toks = []

    def mm(self, out, lhsT, rhs, start, stop, reads, writes, inc=None):
        if inc is None:
            inc = stop
        return self.op("pe", lambda: self.nc.tensor.matmul(out, lhsT, rhs, start=start, stop=stop),
                       reads, writes, inc=inc)

    def act(self, out, in_, func, reads, writes, bias=None, scale=None, accum_out=None):
        kw = {}
        if bias is not None:
            kw["bias"] = bias
        if scale is not None:
            kw["scale"] = scale
        if accum_out is not None:
            kw["accum_out"] = accum_out
        return self.op("act", lambda: self.nc.scalar.activation(out, in_, func, **kw), reads, writes)

    def tt(self, e, out, in0, in1, op, reads, writes):
        return self.op(e, lambda: self.E[e].tensor_tensor(out, in0, in1, op), reads, writes)

    def ts(self, e, out, in0, s1, s2, op0, op1=None, reads=(), writes=()):
        if op1 is None:
            return self.op(e, lambda: self.E[e].tensor_single_scalar(out, in0, s1, op0), reads, writes)
        return self.op(e, lambda: self.E[e].tensor_scalar(out, in0, s1, s2, op0, op1), reads, writes)

    def stt(self, e, out, in0, scalar, in1, op0, op1, reads, writes):
        return self.op(e, lambda: self.E[e].scalar_tensor_tensor(out, in0, scalar, in1, op0, op1), reads, writes)

    def copy(self, e, out, in_, reads, writes):
        if e == "act":
            return self.op(e, lambda: self.nc.scalar.copy(out, in_), reads, writes)
        return self.op(e, lambda: self.E[e].tensor_copy(out, in_), reads, writes)

    def memset(self, e, ap, val, writes):
        return self.op(e, lambda: self.E[e].memset(ap, val), (), writes)


D = 2048
KD = 16
FF = 5632
NF = 44
TT = 1088
TILES = [(0, 512), (512, 512), (1024, 64)]
SEGS = [(0, 512, 0), (512, 512, 0), (1024, 64, 1)]
EPS = 1e-6


def load_x(p, x, xT_dram, tag="x"):
    xv = xT_dram.rearrange("(k q) t -> q k t", q=128)
    for k in range(KD):
        p.dma("sp", x[:, k, :], xv[:, k, :], writes=[(tag, k)])


def store_x(p, x, xT_dram, tag="x"):
    xv = xT_dram.rearrange("(k q) t -> q k t", q=128)
    for k in range(KD):
        p.dma("sp", xv[:, k, :], x[:, k, :], reads=[(tag, k)], is_out=True)


def rms_modulate(p, C, x, h, vec, ig, ishift, iscale, xtag="x", htag="h"):
    nc = p.nc
    sq = C["sq"]
    ones = C["ones_bf"]
    rstd = C["rstd"]
    ab = C["ab"]
    for s in range(2):
        p.stt("dve", ab[:, s, :], vec[:, iscale[s], :], 1.0, vec[:, ig, :], ALU.add, ALU.mult,
              reads=["vec"], writes=[("ab", s)])
    for k in range(KD):
        sl = k % 2
        p.act(sq[:, sl, :], x[:, k, :], AF.Square, reads=[(xtag, k)], writes=[("sq", sl)])
        for ti, (t0, tn) in enumerate(TILES):
            p.mm(C["ps"][4 + ti][:, :tn], ones[:, :], sq[:, sl, t0:t0 + tn], k == 0, k == KD - 1,
                 reads=[("sq", sl), "ones_bf"], writes=[("ps", 4 + ti)], inc=True)
    for ti, (t0, tn) in enumerate(TILES):
        p.ts("dve", rstd[:, t0:t0 + tn], C["ps"][4 + ti][:, :tn], 1.0 / D, EPS, ALU.mult, ALU.add,
             reads=[("ps", 4 + ti)], writes=[("rstd", ti)])
        p.act(rstd[:, t0:t0 + tn], rstd[:, t0:t0 + tn], AF.Sqrt, reads=[("rstd", ti)], writes=[("rstd", ti)])
        p.op("dve", lambda: p.nc.vector.reciprocal(rstd[:, t0:t0 + tn], rstd[:, t0:t0 + tn]),
             reads=[("rstd", ti)], writes=[("rstd", ti)])
    tmp = C["tmp"]
    n = 0
    for k in range(KD):
        for (c0, cn, s) in SEGS:
            sl = n % 2
            n += 1
            p.stt("dve", tmp[:, sl, :cn], x[:, k, c0:c0 + cn], ab[:, s, k:k + 1], rstd[:, c0:c0 + cn],
                  ALU.mult, ALU.mult,
                  reads=[(xtag, k), ("ab", s), ("rstd", 0), ("rstd", 1), ("rstd", 2)], writes=[("tmp", sl)])
            p.act(h[:, k, c0:c0 + cn], tmp[:, sl, :cn], AF.Identity, bias=vec[:, ishift[s], k:k + 1],
                  reads=[("tmp", sl), "vec"], writes=[(htag, k)])


def ffn_main(p, C, x, h, vec, igate, w1, w3, w2, xtag="x", htag="h"):
    nc = p.nc
    NB = NF // 2
    NS = 2
    w1v = w1.rearrange("(k q) f -> q k f", q=128)
    w3v = w3.rearrange("(k q) f -> q k f", q=128)
    w2v = w2.rearrange("(f q) d -> q f d", q=128)
    gh = C["gh"]
    for s in range(2):
        p.ts("dve", gh[:, s, :], vec[:, igate[s], :], 0.5, None, ALU.mult, reads=["vec"], writes=[("gh", s)])
    wb1, wb3, wb2 = C["wb1"], C["wb3"], C["wb2"]
    g = C["g"]
    sil = C["sil"]
    ps = C["ps"]

    def load13(b):
        s = b % NS
        p.dma("pool", wb1[:, s, :, :], w1v[:, :, b * 256:(b + 1) * 256], writes=[("wb1", s)])
        p.dma("pool", wb3[:, s, :, :], w3v[:, :, b * 256:(b + 1) * 256], writes=[("wb3", s)])

    def load2(b):
        s = b % NS
        p.dma("pool", wb2[:, s, :, :], w2v[:, 2 * b:2 * b + 2, :], writes=[("wb2", s)])

    cnt = [0]

    def uv(b):
        s = b % NS
        gs = b % 2
        for f in range(2):
            for ti, (t0, tn) in enumerate(TILES):
                i = cnt[0] % 2
                cnt[0] += 1
                for k in range(KD):
                    p.mm(ps[i][:, :tn], wb1[:, s, k, f * 128:(f + 1) * 128], h[:, k, t0:t0 + tn], k == 0, k == KD - 1,
                         reads=[("wb1", s), (htag, k)], writes=[("ps", i)])
                for k in range(KD):
                    p.mm(ps[2 + i][:, :tn], wb3[:, s, k, f * 128:(f + 1) * 128], h[:, k, t0:t0 + tn], k == 0, k == KD - 1,
                         reads=[("wb3", s), (htag, k)], writes=[("ps", 2 + i)])
                p.act(sil[:, i, :tn], ps[i][:, :tn], AF.Silu, reads=[("ps", i)], writes=[("sil", i)])
                p.tt("dve", g[:, gs, f, t0:t0 + tn], sil[:, i, :tn], ps[2 + i][:, :tn], ALU.mult,
                     reads=[("sil", i), ("ps", 2 + i)], writes=[("g", gs, f, ti)])

    ycnt = [0]

    def yy(b):
        s = b % NS
        gs = b % 2
        for dk in range(KD):
            for ti, (t0, tn) in enumerate(TILES):
                i = 4 + ycnt[0] % 3
                ycnt[0] += 1
                for f in range(2):
                    p.mm(ps[i][:, :tn], wb2[:, s, f, dk * 128:(dk + 1) * 128], g[:, gs, f, t0:t0 + tn], f == 0, f == 1,
                         reads=[("wb2", s), ("g", gs, f, ti)], writes=[("ps", i)])
                sg = 0 if ti < 2 else 1
                p.stt("dve", x[:, dk, t0:t0 + tn], ps[i][:, :tn], gh[:, sg, dk:dk + 1], x[:, dk, t0:t0 + tn],
                      ALU.mult, ALU.add, reads=[("ps", i), ("gh", sg), (xtag, dk)], writes=[(xtag, dk)])

    load13(0)
    load13(1)
    load2(0)
    uv(0)
    for b in range(NB):
        if b + 1 < NB:
            uv(b + 1)
            load2(b + 1)
        yy(b)
        if b + 2 < NB:
            load13(b + 2)


def alloc_ffn(p, nv):
    C = {}
    C["ps"] = [p.ps("ps%d" % i, [128, 512]) for i in range(8)]
    C["ones_bf"] = p.sb("ones_bf", [128, 128], BF16)
    p.memset("dve", C["ones_bf"][:, :], 1.0, ["ones_bf"])
    C["sq"] = p.sb("sq", [128, 2, TT], BF16)
    C["rstd"] = p.sb("rstd", [128, TT], F32)
    C["ab"] = p.sb("ab", [128, 2, 16], F32)
    C["gh"] = p.sb("gh", [128, 2, 16], F32)
    C["tmp"] = p.sb("tmp", [128, 2, 512], F32)
    C["wb1"] = p.sb("wb1", [128, 2, KD, 256], BF16)
    C["wb3"] = p.sb("wb3", [128, 2, KD, 256], BF16)
    C["wb2"] = p.sb("wb2", [128, 2, 2, D], BF16)
    C["g"] = p.sb("g", [128, 2, 2, TT], BF16)
    C["sil"] = p.sb("sil", [128, 2, 512], F32)
    C["vec"] = p.sb("vec", [128, nv, 16], F32)
    return C


def new_nc():
    return bass.Bass("TRN2", target_bir_lowering=False)


def build_mod():
    nc = new_nc()
    cT = nc.dram_tensor("cT", [128, 16, 3], F32, kind="ExternalInput").ap()
    mw = nc.dram_tensor("mw", [4, 2048, 2304], F32, kind="ExternalInput").ap()
    mb = nc.dram_tensor("mb", [128, 4, 18], F32, kind="ExternalInput").ap()
    mo = nc.dram_tensor("mo", [128, 4, 18, 3], F32, kind="ExternalOutput").ap()
    with ExitStack() as st:
        p = Prog(nc, st)
        c_sb = p.sb("c_sb", [128, 16, 3], F32)
        mb_sb = p.sb("mb_sb", [128, 4, 18], F32)
        mo_sb = p.sb("mo_sb", [128, 4, 18, 3], F32)
        wbuf = p.sb("wbuf", [128, 2, 16, 768], F32)
        ps = [p.ps("ps%d" % i, [128, 512]) for i in range(4)]
        p.dma("sp", c_sb[:], cT, writes=["c"])
        p.dma("sp", mb_sb[:], mb, writes=["mb"])
        sg = p.sb("sg", [128, 16, 3], F32)
        p.act(sg[:], c_sb[:], AF.Silu, reads=["c"], writes=["sg"])
        n = 0
        for i in range(4):
            mwv = mw[i].rearrange("(k q) f -> q k f", q=128)
            for gidx in range(3):
                s = n % 2
                n += 1
                p.dma("sp", wbuf[:, s, :, :], mwv[:, :, gidx * 768:(gidx + 1) * 768], writes=[("wbuf", s)])
                for jj in range(6):
                    j = gidx * 6 + jj
                    pi = j % 4
                    for k in range(16):
                        p.mm(ps[pi][:, 0:3], wbuf[:, s, k, jj * 128:(jj + 1) * 128], sg[:, k, :], k == 0, k == 15,
                             reads=[("wbuf", s), "sg"], writes=[("ps", pi)])
                    p.act(mo_sb[:, i, j, :], ps[pi][:, 0:3], AF.Identity, bias=mb_sb[:, i, j:j + 1],
                          reads=[("ps", pi), "mb"], writes=["mo"])
        p.dma("sp", mo, mo_sb[:], reads=["mo"], is_out=True)
        p.finish()
    return nc


def build_A():
    nc = new_nc()
    xT = nc.dram_tensor("xT", [2048, TT], F32, kind="ExternalInput").ap()
    vecd = nc.dram_tensor("vecd", [128, 12, 16], F32, kind="ExternalInput").ap()
    w1 = nc.dram_tensor("w1", [2048, 5632], F32, kind="ExternalInput").ap()
    w3 = nc.dram_tensor("w3", [2048, 5632], F32, kind="ExternalInput").ap()
    w2 = nc.dram_tensor("w2", [5632, 2048], F32, kind="ExternalInput").ap()
    xo = nc.dram_tensor("xo", [2048, TT], F32, kind="ExternalOutput").ap()
    ho = nc.dram_tensor("ho", [2048, TT], BF16, kind="ExternalOutput").ap()
    with ExitStack() as st:
        p = Prog(nc, st)
        C = alloc_ffn(p, 12)
        x = p.sb("x", [128, 16, TT], F32)
        h = p.sb("h", [128, 16, TT], BF16)
        p.dma("sp", C["vec"][:], vecd, writes=["vec"])
        load_x(p, x, xT)
        rms_modulate(p, C, x, h, C["vec"], 6, (0, 3), (1, 4))
        ffn_main(p, C, x, h, C["vec"], (2, 5), w1, w3, w2)
        store_x(p, x, xo)
        rms_modulate(p, C, x, h, C["vec"], 11, (7, 9), (8, 10))
        hv = ho.rearrange("(k q) t -> q k t", q=128)
        for k in range(KD):
            p.dma("sp", hv[:, k, :], h[:, k, :], reads=[("h", k)], is_out=True)
        p.finish()
    return nc


def build_C(even, do_ffn=True):
    nc = new_nc()
    xT = nc.dram_tensor("xT", [2048, TT], F32, kind="ExternalInput").ap()
    mixT = nc.dram_tensor("mixT", [2048, TT], F32, kind="ExternalInput").ap()
    vecd = nc.dram_tensor("vecd", [128, 10, 16], F32, kind="ExternalInput").ap()
    W = nc.dram_tensor("W", [2048, 2048], F32, kind="ExternalInput").ap()
    w1 = nc.dram_tensor("w1", [2048, 5632], F32, kind="ExternalInput").ap()
    w3 = nc.dram_tensor("w3", [2048, 5632], F32, kind="ExternalInput").ap()
    w2 = nc.dram_tensor("w2", [5632, 2048], F32, kind="ExternalInput").ap()
    xo = nc.dram_tensor("xo", [2048, TT], F32, kind="ExternalOutput").ap()
    with ExitStack() as st:
        p = Prog(nc, st)
        C = alloc_ffn(p, 10)
        x = p.sb("x", [128, 16, TT], F32)
        h = p.sb("h", [128, 16, TT], BF16)
        stg = p.sb("stg", [128, 2, TT], F32)
        vec = C["vec"]
        ps = C["ps"]
        p.dma("sp", vec[:], vecd, writes=["vec"])
        load_x(p, x, xT)
        mv = mixT.rearrange("(k q) t -> q k t", q=128)
        if not even:
            for k in range(KD):
                s = k % 2
                p.dma("sp", stg[:, s, :], mv[:, k, :], writes=[("stg", s)])
                p.copy("act" if k % 2 else "dve", h[:, k, :], stg[:, s, :], reads=[("stg", s)], writes=[("h", k)])
        else:
            for k in range(8):
                s = k % 2
                p.dma("sp", stg[:, s, :], mv[:, k, :], writes=[("stg", s)])
                p.copy("act" if k % 2 else "dve", h[:, k, :], stg[:, s, :], reads=[("stg", s)], writes=[("h", k)])
            sq = C["sq"]
            rstd = C["rstd"]
            for grp in range(2):
                for kk in range(4):
                    k = 8 + grp * 4 + kk
                    s = k % 2
                    p.dma("sp", stg[:, s, :], mv[:, k, :], writes=[("stg", s)])
                    p.act(sq[:, s, :], stg[:, s, :], AF.Square, reads=[("stg", s)], writes=[("sq", s)])
                    for ti, (t0, tn) in enumerate(TILES):
                        p.mm(ps[4 + ti][:, :tn], C["ones_bf"][:, :], sq[:, s, t0:t0 + tn], kk == 0, kk == 3,
                             reads=[("sq", s), "ones_bf"], writes=[("ps", 4 + ti)], inc=True)
                for ti, (t0, tn) in enumerate(TILES):
                    p.ts("dve", rstd[:, t0:t0 + tn], ps[4 + ti][:, :tn], 1.0 / 512, EPS, ALU.mult, ALU.add,
                         reads=[("ps", 4 + ti)], writes=[("rstd", ti)])
                    p.act(rstd[:, t0:t0 + tn], rstd[:, t0:t0 + tn], AF.Sqrt, reads=[("rstd", ti)], writes=[("rstd", ti)])
                    p.op("dve", lambda: nc.vector.reciprocal(rstd[:, t0:t0 + tn], rstd[:, t0:t0 + tn]),
                         reads=[("rstd", ti)], writes=[("rstd", ti)])
                for kk in range(4):
                    k = 8 + grp * 4 + kk
                    s = k % 2
                    p.dma("sp", stg[:, s, :], mv[:, k, :], writes=[("stg", s)])
                    p.stt("dve", h[:, k, :], stg[:, s, :], vec[:, 9, k:k + 1], rstd[:, :], ALU.mult, ALU.mult,
                          reads=[("stg", s), "vec", ("rstd", 0), ("rstd", 1), ("rstd", 2)], writes=[("h", k)])
        Wv = W.rearrange("(k q) f -> q k f", q=128)
        ring = [(C["wb1"], 0, "wb1"), (C["wb1"], 1, "wb1"), (C["wb3"], 0, "wb3"), (C["wb3"], 1, "wb3")]
        n = 0
        for gi in range(8):
            buf, s, nm = ring[gi % 4]
            p.dma("pool", buf[:, s, :, :], Wv[:, :, gi * 256:(gi + 1) * 256], writes=[(nm, s)])
            for jj in range(2):
                dk = gi * 2 + jj
                for ti, (t0, tn) in enumerate(TILES):
                    pi = n % 4
                    n += 1
                    for k in range(KD):
                        p.mm(ps[pi][:, :tn], buf[:, s, k, jj * 128:(jj + 1) * 128], h[:, k, t0:t0 + tn], k == 0, k == KD - 1,
                             reads=[(nm, s), ("h", k)], writes=[("ps", pi)])
                    sg = 0 if ti < 2 else 1
                    p.stt("dve", x[:, dk, t0:t0 + tn], ps[pi][:, :tn], vec[:, 7 + sg, dk:dk + 1], x[:, dk, t0:t0 + tn],
                          ALU.mult, ALU.add, reads=[("ps", pi), "vec", ("x", dk)], writes=[("x", dk)])
        if do_ffn:
            rms_modulate(p, C, x, h, vec, 6, (0, 3), (1, 4))
            ffn_main(p, C, x, h, vec, (2, 5), w1, w3, w2)
        store_x(p, x, xo)
        p.finish()
    return nc


def build_fourier():
    nc = new_nc()
    L, CG, LC = 4096, 512, 256
    hg = nc.dram_tensor("hg", [CG, L], BF16, kind="ExternalInput").ap()
    hcg = nc.dram_tensor("hcg", [CG, LC], BF16, kind="ExternalInput").ap()
    CCd = nc.dram_tensor("CC", [128, 4, 512], BF16, kind="ExternalInput").ap()
    SCd = nc.dram_tensor("SC", [128, 4, 512], BF16, kind="ExternalInput").ap()
    CLt = nc.dram_tensor("CLt", [16, 128, 32, 256], BF16, kind="ExternalInput").ap()
    SLt = nc.dram_tensor("SLt", [16, 128, 32, 256], BF16, kind="ExternalInput").ap()
    CLc = nc.dram_tensor("CLc", [128, 2, 256], BF16, kind="ExternalInput").ap()
    SLc = nc.dram_tensor("SLc", [128, 2, 256], BF16, kind="ExternalInput").ap()
    FT = nc.dram_tensor("FT", [CG, L], F32, kind="ExternalOutput").ap()
    FcT = nc.dram_tensor("FcT", [CG, LC], F32, kind="ExternalOutput").ap()
    with ExitStack() as st:
        p = Prog(nc, st)
        ps = [p.ps("ps%d" % i, [128, 512]) for i in range(8)]
        h_sb = p.sb("h_sb", [128, 4, L], BF16)
        hc_sb = p.sb("hc_sb", [128, 4, LC], BF16)
        cc = p.sb("cc", [128, 4, 512], BF16)
        sc = p.sb("sc", [128, 4, 512], BF16)
        U = p.sb("U", [128, 32, 512], BF16)
        V = p.sb("V", [128, 32, 512], BF16)
        Uc = p.sb("Uc", [128, 2, 512], BF16)
        Vc = p.sb("Vc", [128, 2, 512], BF16)
        clc = p.sb("clc", [128, 2, 256], BF16)
        slc = p.sb("slc", [128, 2, 256], BF16)
        clb = p.sb("clb", [128, 2, 32, 256], BF16)
        slb = p.sb("slb", [128, 2, 32, 256], BF16)
        ob = p.sb("ob", [128, 4, 256], F32)
        hv = hg.rearrange("(k q) t -> q k t", q=128)
        for k in range(4):
            p.dma("sp", h_sb[:, k, :], hv[:, k, :], writes=[("h", k)])
        p.dma("sp", hc_sb[:], hcg.rearrange("(k q) t -> q k t", q=128), writes=["hc"])
        p.dma("sp", cc[:], CCd, writes=["cc"])
        p.dma("sp", sc[:], SCd, writes=["sc"])
        p.dma("sp", clc[:], CLc, writes=["clc"])
        p.dma("sp", slc[:], SLc, writes=["slc"])

        def step1(src, stag, nl, Uo, Vo, utag):
            for lc in range(nl):
                i = lc % 2
                for k in range(4):
                    p.mm(ps[i][:, :], src[:, k, lc * 128:(lc + 1) * 128], cc[:, k, :], k == 0, k == 3,
                         reads=[stag if isinstance(stag, str) else (stag[0], k), "cc"], writes=[("ps", i)])
                for k in range(4):
                    p.mm(ps[2 + i][:, :], src[:, k, lc * 128:(lc + 1) * 128], sc[:, k, :], k == 0, k == 3,
                         reads=[stag if isinstance(stag, str) else (stag[0], k), "sc"], writes=[("ps", 2 + i)])
                p.copy("dve", Uo[:, lc, :], ps[i][:, :], reads=[("ps", i)], writes=[(utag + "U", lc)])
                p.op("act", lambda: nc.scalar.mul(Vo[:, lc, :], ps[2 + i][:, :], -1.0),
                     reads=[("ps", 2 + i)], writes=[(utag + "V", lc)])

        step1(hc_sb, "hc", 2, Uc, Vc, "c")
        for cchunk in range(4):
            i = 4 + cchunk % 2
            for lc in range(2):
                p.mm(ps[i][:, :256], Uc[:, lc, cchunk * 128:(cchunk + 1) * 128], clc[:, lc, :], lc == 0, False,
                     reads=[("cU", lc), "clc"], writes=[("ps", i)], inc=False)
                p.mm(ps[i][:, :256], Vc[:, lc, cchunk * 128:(cchunk + 1) * 128], slc[:, lc, :], False, lc == 1,
                     reads=[("cV", lc), "slc"], writes=[("ps", i)], inc=(lc == 1))
            p.op("act", lambda: nc.scalar.mul(ob[:, cchunk, :], ps[i][:, :256], float((LC * CG) ** -0.5)),
                 reads=[("ps", i)], writes=[("ob", cchunk)])
            p.dma("sp", FcT[cchunk * 128:(cchunk + 1) * 128, :], ob[:, cchunk, :], reads=[("ob", cchunk)], is_out=True)
        step1(h_sb, ("h",), 32, U, V, "l")
        n = 0
        for t in range(16):
            s = t % 2
            p.dma("sp", clb[:, s, :, :], CLt[t], writes=[("clb", s)])
            p.dma("sp", slb[:, s, :, :], SLt[t], writes=[("slb", s)])
            for cchunk in range(4):
                i = 4 + n % 4
                oi = n % 4
                n += 1
                for lc in range(32):
                    p.mm(ps[i][:, :256], U[:, lc, cchunk * 128:(cchunk + 1) * 128], clb[:, s, lc, :], lc == 0, False,
                         reads=[("lU", lc), ("clb", s)], writes=[("ps", i)], inc=False)
                    p.mm(ps[i][:, :256], V[:, lc, cchunk * 128:(cchunk + 1) * 128], slb[:, s, lc, :], False, lc == 31,
                         reads=[("lV", lc), ("slb", s)], writes=[("ps", i)], inc=(lc == 31))
                p.op("act", lambda: nc.scalar.mul(ob[:, oi, :], ps[i][:, :256], float((L * CG) ** -0.5)),
                     reads=[("ps", i)], writes=[("ob", oi)])
                p.dma("sp", FT[cchunk * 128:(cchunk + 1) * 128, t * 256:(t + 1) * 256], ob[:, oi, :],
                      reads=[("ob", oi)], is_out=True)
        p.finish()
    return nc


NA_SCALE = 128 ** -0.5
TOK = 4352


def na_blocks():
    out = []
    for i in range(16):
        js = [j for j in range(2 * i - 2, 2 * i + 4) if 0 <= j <= 31]
        cls = 0 if i == 0 else (2 if i == 15 else 1)
        out.append(([(j, j - 2 * i + 2) for j in js], cls))
    return out


def build_na():
    nc = new_nc()
    hT = nc.dram_tensor("hT", [2048, TOK], BF16, kind="ExternalInput").ap()
    wq = nc.dram_tensor("wq", [2048, 256], F32, kind="ExternalInput").ap()
    wk = nc.dram_tensor("wk", [2048, 256], F32, kind="ExternalInput").ap()
    wv = nc.dram_tensor("wv", [2048, 256], F32, kind="ExternalInput").ap()
    qkg = nc.dram_tensor("qkg", [128, 2], F32, kind="ExternalInput").ap()
    cosd = nc.dram_tensor("cosT", [128, 4096], F32, kind="ExternalInput").ap()
    sind = nc.dram_tensor("sinT", [128, 4096], F32, kind="ExternalInput").ap()
    protd = nc.dram_tensor("prot", [128, 128], F32, kind="ExternalInput").ap()
    biasd = nc.dram_tensor("biasblk", [128, 2, 6, 256], F32, kind="ExternalInput").ap()
    maskd = nc.dram_tensor("mask01", [128, 3, 6, 256], BF16, kind="ExternalInput").ap()
    attn = nc.dram_tensor("attnT", [256, TOK], F32, kind="ExternalOutput").ap()
    with ExitStack() as st:
        p = Prog(nc, st)
        ps = [p.ps("ps%d" % i, [128, 512]) for i in range(8)]
        wsb = [p.sb("wsb_%s" % n, [128, 16, 256], BF16) for n in "qkv"]
        g_sb = p.sb("g_sb", [128, 2], F32)
        prot = p.sb("prot_sb", [128, 128], F32)
        bias = p.sb("bias_sb", [128, 2, 6, 256], F32)
        mask = p.sb("mask_sb", [128, 3, 6, 256], BF16)
        ones = p.sb("ones_bf", [128, 128], BF16)
        QR = p.sb("QR", [128, 2, 4096], BF16)
        QP = p.sb("QP", [128, 2, TOK], BF16)
        KK = p.sb("KK", [128, 2, TOK], BF16)
        Vt = p.sb("Vt", [128, 34, 256], BF16)
        hb = p.sb("hb", [128, 2, 16, 256], BF16)
        cs = p.sb("cs", [128, 2, 2, 256], F32)
        wk_ = p.sb("work", [128, 8, 256], F32)
        sqb = p.sb("sqb", [128, 2, 256], BF16)
        Eb = p.sb("Eb", [128, 4, 256], BF16)
        Em = p.sb("Em", [128, 4, 256], BF16)
        ob = p.sb("ob", [128, 2, 256], F32)
        p.memset("dve", ones[:, :], 1.0, ["ones"])
        for n, (w, d) in enumerate(zip(wsb, (wq, wk, wv))):
            p.dma("pool", w[:], d.rearrange("(k q) f -> q k f", q=128), writes=[("w", n)])
        p.dma("sp", g_sb[:], qkg, writes=["g"])
        p.dma("sp", prot[:], protd, writes=["prot"])
        p.dma("sp", bias[:], biasd, writes=["bias"])
        p.dma("sp", mask[:], maskd, writes=["mask"])
        hv = hT.rearrange("(k q) t -> q k t", q=128)
        wn = [0]

        def wtile():
            i = wn[0] % 8
            wn[0] += 1
            return i

        pn = [0]

        def pbank():
            i = pn[0] % 8
            pn[0] += 1
            return i

        for t in range(17):
            s = t % 2
            c0 = t * 256
            lat = t >= 1
            p.dma("sp", hb[:, s, :, :], hv[:, :, c0:c0 + 256], writes=[("hb", s)])
            if lat:
                l0 = c0 - 256
                p.dma("sp", cs[:, s, 0, :], cosd[:, l0:l0 + 256], writes=[("cs", s, 0)])
                p.dma("sp", cs[:, s, 1, :], sind[:, l0:l0 + 256], writes=[("cs", s, 1)])
            for half in range(2):
                b = pbank()
                for k in range(16):
                    p.mm(ps[b][:, :256], hb[:, s, k, half * 128:(half + 1) * 128], wsb[2][:, k, :], k == 0, k == 15,
                         reads=[("hb", s), ("w", 2)], writes=[("ps", b)])
                p.copy("act", Vt[:, 2 * t + half, :], ps[b][:, :256], reads=[("ps", b)], writes=[("Vt", 2 * t + half)])
            for e in range(2):
                for qk in range(2):
                    b = pbank()
                    for k in range(16):
                        p.mm(ps[b][:, :256], wsb[qk][:, k, e * 128:(e + 1) * 128], hb[:, s, k, :], k == 0, k == 15,
                             reads=[("hb", s), ("w", qk)], writes=[("ps", b)])
                    sl = (2 * e + qk) % 2
                    p.act(sqb[:, sl, :], ps[b][:, :256], AF.Square, reads=[("ps", b)], writes=[("sqb", sl)])
                    b2 = pbank()
                    p.mm(ps[b2][:, :256], ones[:, :], sqb[:, sl, :], True, True, reads=["ones", ("sqb", sl)], writes=[("ps", b2)])
                    w_r = wtile()
                    p.ts("dve", wk_[:, w_r, :], ps[b2][:, :256], 1.0 / 128, EPS, ALU.mult, ALU.add,
                         reads=[("ps", b2)], writes=[("wk", w_r)])
                    p.act(wk_[:, w_r, :], wk_[:, w_r, :], AF.Sqrt, reads=[("wk", w_r)], writes=[("wk", w_r)])
                    p.op("dve", lambda: nc.vector.reciprocal(wk_[:, w_r, :], wk_[:, w_r, :]),
                         reads=[("wk", w_r)], writes=[("wk", w_r)])
                    w_n = wtile()
                    p.stt("dve", wk_[:, w_n, :], ps[b][:, :256], g_sb[:, qk:qk + 1], wk_[:, w_r, :], ALU.mult, ALU.mult,
                          reads=[("ps", b), "g", ("wk", w_r)], writes=[("wk", w_n)])
                    if qk == 0:
                        p.copy("act", QP[:, e, c0:c0 + 256], wk_[:, w_n, :], reads=[("wk", w_n)], writes=[("QP", e, t)])
                    elif not lat:
                        p.copy("act", KK[:, e, c0:c0 + 256], wk_[:, w_n, :], reads=[("wk", w_n)], writes=[("KK", e, t)])
                    if lat:
                        b3 = pbank()
                        p.mm(ps[b3][:, :256], prot[:, :], wk_[:, w_n, :], True, True, reads=["prot", ("wk", w_n)], writes=[("ps", b3)])
                        w_a = wtile()
                        p.tt("pool", wk_[:, w_a, :], wk_[:, w_n, :], cs[:, s, 0, :], ALU.mult,
                             reads=[("wk", w_n), ("cs", s, 0)], writes=[("wk", w_a)])
                        w_b = wtile()
                        p.tt("dve", wk_[:, w_b, :], ps[b3][:, :256], cs[:, s, 1, :], ALU.mult,
                             reads=[("ps", b3), ("cs", s, 1)], writes=[("wk", w_b)])
                        dst = QR[:, e, l0:l0 + 256] if qk == 0 else KK[:, e, c0:c0 + 256]
                        dtag = ("QR", e, t) if qk == 0 else ("KK", e, t)
                        p.tt("pool", dst, wk_[:, w_a, :], wk_[:, w_b, :], ALU.add,
                             reads=[("wk", w_a), ("wk", w_b)], writes=[dtag])

        blocks = na_blocks()
        en = [0]

        def attend(e, qtile_lat, keys, cls, out_c0):
            bo = 4 + (en[0] % 2)
            bd = 6 + (en[0] % 2)
            oi = en[0] % 2
            en[0] += 1
            nk = len(keys)
            for n, key in enumerate(keys):
                b = pbank() % 4
                ei = pn[0] % 4
                if key[0] == "w":
                    _, j, sidx = key
                    t_k = 1 + j // 2
                    i = qtile_lat
                    p.mm(ps[b][:, :256], KK[:, e, 256 + j * 128:256 + (j + 1) * 128], QR[:, e, i * 256:(i + 1) * 256], True, True,
                         reads=[("KK", e, t_k), ("QR", e, 1 + i)], writes=[("ps", b)])
                    w_t = wtile()
                    p.stt("dve", wk_[:, w_t, :], ps[b][:, :256], NA_SCALE, bias[:, e, sidx, :], ALU.mult, ALU.add,
                          reads=[("ps", b), "bias"], writes=[("wk", w_t)])
                    p.act(Eb[:, ei, :], wk_[:, w_t, :], AF.Exp, reads=[("wk", w_t)], writes=[("Eb", ei)])
                    p.tt("pool", Em[:, ei, :], Eb[:, ei, :], mask[:, cls, sidx, :], ALU.mult,
                         reads=[("Eb", ei), "mask"], writes=[("Em", ei)])
                    vch = 2 + j
                else:
                    cj = key[1]
                    qsrc = QP[:, e, 0:256] if qtile_lat is None else QP[:, e, 256 + qtile_lat * 256:256 + (qtile_lat + 1) * 256]
                    qtag = ("QP", e, 0) if qtile_lat is None else ("QP", e, 1 + qtile_lat)
                    p.mm(ps[b][:, :256], KK[:, e, cj * 128:(cj + 1) * 128], qsrc, True, True,
                         reads=[("KK", e, 0), qtag], writes=[("ps", b)])
                    p.act(Em[:, ei, :], ps[b][:, :256], AF.Exp, scale=NA_SCALE, reads=[("ps", b)], writes=[("Em", ei)])
                    vch = cj
                p.mm(ps[bo][:, :256], Vt[:, vch, e * 128:(e + 1) * 128], Em[:, ei, :], n == 0, n == nk - 1,
                     reads=[("Vt", vch), ("Em", ei)], writes=[("ps", bo)], inc=True)
                p.mm(ps[bd][:, :256], ones[:, :], Em[:, ei, :], n == 0, n == nk - 1,
                     reads=["ones", ("Em", ei)], writes=[("ps", bd)], inc=True)
            w_d = wtile()
            p.op("dve", lambda: nc.vector.reciprocal(wk_[:, w_d, :], ps[bd][:, :256]), reads=[("ps", bd)], writes=[("wk", w_d)])
            p.tt("dve", ob[:, oi, :], ps[bo][:, :256], wk_[:, w_d, :], ALU.mult,
                 reads=[("ps", bo), ("wk", w_d)], writes=[("ob", oi)])
            p.dma("sp", attn[e * 128:(e + 1) * 128, out_c0:out_c0 + 256], ob[:, oi, :], reads=[("ob", oi)], is_out=True)

        for e in range(2):
            attend(e, None, [("c", 0), ("c", 1)], 0, 0)
            for i in range(16):
                wl, cls = blocks[i]
                keys = [("w", j, sidx) for (j, sidx) in wl] + [("c", 0), ("c", 1)]
                attend(e, i, keys, cls, 256 + i * 256)
        p.finish()
    return nc


RAWW = 4360


def build_ssd():
    nc = new_nc()
    hT = nc.dram_tensor("hT", [2048, TOK], BF16, kind="ExternalInput").ap()
    wxd = nc.dram_tensor("wx", [2048, 256], F32, kind="ExternalInput").ap()
    wbcd = nc.dram_tensor("wbc", [2048, 256], F32, kind="ExternalInput").ap()
    wzd = nc.dram_tensor("wz", [2048, 256], F32, kind="ExternalInput").ap()
    wdtd = nc.dram_tensor("wdt", [2048, 8], F32, kind="ExternalInput").ap()
    cwd = nc.dram_tensor("convw", [128, 4, 5], F32, kind="ExternalInput").ap()
    cbd = nc.dram_tensor("convb", [128, 4], F32, kind="ExternalInput").ap()
    dtbd = nc.dram_tensor("dtb", [128, 34, 8], F32, kind="ExternalInput").ap()
    alogd = nc.dram_tensor("alog", [128, 34, 8], F32, kind="ExternalInput").ap()
    Dd = nc.dram_tensor("Drep", [128, 256], F32, kind="ExternalInput").ap()
    trid = nc.dram_tensor("tri", [128, 2, 128], F32, kind="ExternalInput").ap()
    identd = nc.dram_tensor("ident", [128, 128], F32, kind="ExternalInput").ap()
    yz = nc.dram_tensor("yz", [TOK, 256], F32, kind="ExternalOutput").ap()
    Yf = nc.dram_tensor("Yf_scr", [TOK, 256], F32, kind="Internal").ap()
    with ExitStack() as st:
        p = Prog(nc, st)
        st1 = ExitStack()
        ps = [p.ps("ps%d" % i, [128, 512]) for i in range(8)]
        wz = p.sb("wz_sb", [128, 16, 256], BF16)
        cw = p.sb("cw", [128, 4, 5], F32)
        cb = p.sb("cb", [128, 4], F32)
        Drep = p.sb("Drep_sb", [128, 256], F32)
        tri = p.sb("tri_sb", [128, 2, 128], F32)
        ident = p.sb("ident_sb", [128, 128], F32)
        ones = p.sb("ones_f", [128, 128], F32)
        BT = p.sb("BT", [128, TOK], BF16)
        CT = p.sb("CT", [128, TOK], BF16)
        Btok = p.sb("Btok", [128, 34, 128], BF16)
        xs = p.sb("xs_tok", [128, 34, 256], F32)
        dt = p.sb("dt", [128, 34, 8], F32)
        aa = p.sb("aa", [128, 34, 8], F32)
        wx = p.sb("wx_sb", [128, 16, 256], BF16, stack=st1)
        wbc = p.sb("wbc_sb", [128, 16, 256], BF16, stack=st1)
        wdt = p.sb("wdt_sb", [128, 16, 8], BF16, stack=st1)
        dtb = p.sb("dtb_sb", [128, 34, 8], F32, stack=st1)
        alog = p.sb("alog_sb", [128, 34, 8], F32, stack=st1)
        raw = p.sb("raw", [128, 2, RAWW], F32, stack=st1)
        hb = p.sb("hb", [128, 2, 16, 256], BF16, stack=st1)
        t8 = p.sb("t8", [128, 2, 34, 8], F32, stack=st1)
        cacc = p.sb("cacc", [128, 2, 512], F32, stack=st1)
        cout = p.sb("cout", [128, 2, 512], F32, stack=st1)
        p.memset("dve", ones[:, :], 1.0, ["ones"])
        p.memset("dve", raw[:, :, :], 0.0, [("raw", 0), ("raw", 1)])
        for (w, d, n) in ((wx, wxd, "wx"), (wbc, wbcd, "wbc"), (wz, wzd, "wz"), (wdt, wdtd, "wdt")):
            p.dma("pool", w[:], d.rearrange("(k q) f -> q k f", q=128), writes=[n])
        for (sbt, d, n) in ((cw, cwd, "cw"), (cb, cbd, "cb"), (dtb, dtbd, "dtb"), (alog, alogd, "alog"), (Drep, Dd, "Drep"),
                            (tri, trid, "tri"), (ident, identd, "ident")):
            p.dma("sp", sbt[:], d, writes=[n])
        hv = hT.rearrange("(k q) t -> q k t", q=128)
        pn = [0]

        def pbank(lo=0, hi=8):
            i = lo + pn[0] % (hi - lo)
            pn[0] += 1
            return i

        def rawoff(tok):
            return 2 + tok if tok < 256 else 262 + (tok - 256)

        for pas in range(2):
            wsrc, wtag = (wx, "wx") if pas == 0 else (wbc, "wbc")
            for t in range(17):
                s = t % 2
                c0 = t * 256
                p.dma("sp", hb[:, s, :, :], hv[:, :, c0:c0 + 256], writes=[("hb", s)])
                for ci in range(2):
                    b = pbank()
                    for k in range(16):
                        p.mm(ps[b][:, :256], wsrc[:, k, ci * 128:(ci + 1) * 128], hb[:, s, k, :], k == 0, k == 15,
                             reads=[("hb", s), wtag], writes=[("ps", b)])
                    ro = rawoff(c0)
                    p.copy("act" if ci else "dve", raw[:, ci, ro:ro + 256], ps[b][:, :256], reads=[("ps", b)], writes=[("raw", ci)])
                if pas == 0:
                    for half in range(2):
                        b = pbank()
                        for k in range(16):
                            p.mm(ps[b][:, :8], hb[:, s, k, half * 128:(half + 1) * 128], wdt[:, k, :], k == 0, k == 15,
                                 reads=[("hb", s), "wdt"], writes=[("ps", b)])
                        p.copy("dve", dt[:, 2 * t + half, :], ps[b][:, :8], reads=[("ps", b)], writes=["dt"])
            tiles = [(0, 256)] + [(256 + 512 * i, 512) for i in range(8)]
            for ci in range(2):
                cidx = 2 * pas + ci
                for ti, (tk0, tn) in enumerate(tiles):
                    sl = ti % 2
                    ro = rawoff(tk0)
                    p.ts("dve", cacc[:, sl, :tn], raw[:, ci, ro - 2:ro - 2 + tn], cw[:, cidx, 0:1], None, ALU.mult,
                         reads=[("raw", ci), "cw"], writes=[("cacc", sl)])
                    for kk in range(1, 5):
                        p.stt("dve", cacc[:, sl, :tn], raw[:, ci, ro - 2 + kk:ro - 2 + kk + tn], cw[:, cidx, kk:kk + 1],
                              cacc[:, sl, :tn], ALU.mult, ALU.add, reads=[("raw", ci), "cw", ("cacc", sl)], writes=[("cacc", sl)])
                    p.act(cout[:, sl, :tn], cacc[:, sl, :tn], AF.Silu, bias=cb[:, cidx:cidx + 1],
                          reads=[("cacc", sl), "cb"], writes=[("cout", sl)])
                    if pas == 1:
                        dst = BT if ci == 0 else CT
                        p.copy("pool", dst[:, tk0:tk0 + tn], cout[:, sl, :tn], reads=[("cout", sl)], writes=[("BT" if ci == 0 else "CT", ti)])
                    if pas == 0 or ci == 0:
                        for j in range(tn // 128):
                            b = pbank()
                            ch = (tk0 + j * 128) // 128
                            p.op("pe", lambda: nc.tensor.transpose(ps[b][:, :128], cout[:, sl, j * 128:(j + 1) * 128], ident[:, :]),
                                 reads=[("cout", sl), "ident"], writes=[("ps", b)], inc=True)
                            if pas == 0:
                                p.copy("act", xs[:, ch, ci * 128:(ci + 1) * 128], ps[b][:, :128], reads=[("ps", b)], writes=[("xs", ch)])
                            else:
                                p.copy("act", Btok[:, ch, :], ps[b][:, :128], reads=[("ps", b)], writes=[("Btok", ch)])
            if pas == 0:
                p.tt("dve", dt[:, :, :], dt[:, :, :], dtb[:, :, :], ALU.add, reads=["dt", "dtb"], writes=["dt"])
                p.act(t8[:, 0, :, :], dt[:, :, :], AF.Abs, reads=["dt"], writes=[("t8", 0)])
                p.act(t8[:, 0, :, :], t8[:, 0, :, :], AF.Exp, scale=-1.0, reads=[("t8", 0)], writes=[("t8", 0)])
                p.act(t8[:, 0, :, :], t8[:, 0, :, :], AF.Ln, bias=1.0, reads=[("t8", 0)], writes=[("t8", 0)])
                p.stt("dve", dt[:, :, :], dt[:, :, :], 0.0, t8[:, 0, :, :], ALU.max, ALU.add, reads=["dt", ("t8", 0)], writes=["dt"])
                p.act(t8[:, 1, :, :], alog[:, :, :], AF.Exp, reads=["alog"], writes=[("t8", 1)])
                p.stt("dve", aa[:, :, :], t8[:, 1, :, :], -1.0, dt[:, :, :], ALU.mult, ALU.mult, reads=[("t8", 1), "dt"], writes=["aa"])

        p.barrier()
        st1.close()
        H = p.sb("H", [128, 256], F32)
        Hbf = p.sb("Hbf", [128, 256], BF16)
        cs_sb = p.sb("cs_sb", [128, 2, 4], F32)
        AT = p.sb("AT", [128, 2, 512], F32)
        sm = p.sb("sm", [128, 2, 3, 4], F32)
        dsc = p.sb("dsc", [128, 2, 2, 4], F32)
        Dm = p.sb("Dm", [128, 2, 512], F32)
        Ed = p.sb("Ed", [128, 2, 512], F32)
        ER = p.sb("ER", [128, 2, 512], F32)
        Gm = p.sb("Gm", [128, 2, 128], F32)
        LT = p.sb("LT", [128, 2, 512], BF16)
        CE = p.sb("CE", [128, 2, 512], BF16)
        xdt = p.sb("xdt", [128, 2, 256], BF16)
        xdd = p.sb("xdd", [128, 2, 256], BF16)
        ysb = p.sb("ysb", [128, 2, 256], F32)
        yfl = p.sb("yfl", [128, 2, 256], F32)
        zs = p.sb("zs", [128, 2, 256], F32)
        hz = p.sb("hz", [128, 2, 16, 128], BF16)
        step = [0]

        def chunk_step(c, d, first, final_pass):
            s = step[0] % 2
            step[0] += 1
            tk = c * 128
            last = 127 if d == 0 else 0
            a_d = aa[:, c, 4 * d:4 * d + 4]
            if first:
                p.memset("pool", H[:, :], 0.0, ["H"])
                p.memset("pool", Hbf[:, :], 0.0, ["Hbf"])
            gb = 2 + s
            p.mm(ps[gb][:, 128:132], tri[:, d, :], a_d, True, True, reads=["tri", "aa"], writes=[("ps", gb)])
            p.copy("dve", cs_sb[:, s, :], ps[gb][:, 128:132], reads=[("ps", gb)], writes=[("cs", s)])
            for e in range(4):
                p.ts("pool" if e % 2 else "dve", AT[:, s, e * 128:(e + 1) * 128], tri[:, d, :], aa[:, c, 4 * d + e:4 * d + e + 1], None, ALU.mult,
                     reads=["tri", "aa"], writes=[("AT", s, e)])
            rb = s
            p.mm(ps[rb][:, :], ones[:, :], AT[:, s, :], True, True, reads=["ones"] + [("AT", s, e) for e in range(4)], writes=[("ps", rb)])
            Rv = ps[rb][:, :].rearrange("q (e i) -> q e i", e=4)
            p.tt("dve", sm[:, s, 0, :], Rv[:, :, last], cs_sb[:, s, :], ALU.subtract, reads=[("ps", rb), ("cs", s)], writes=[("sm", s, 0)])
            p.act(sm[:, s, 1, :], sm[:, s, 0, :], AF.Exp, reads=[("sm", s, 0)], writes=[("sm", s, 1)])
            p.act(sm[:, s, 2, :], Rv[:, :, last], AF.Exp, reads=[("ps", rb)], writes=[("sm", s, 2)])
            for e in range(4):
                p.ts("dve", Dm[:, s, e * 128:(e + 1) * 128], ps[rb][:, e * 128:(e + 1) * 128], cs_sb[:, s, e:e + 1], 0.0, ALU.subtract, ALU.min,
                     reads=[("ps", rb), ("cs", s)], writes=[("Dm", s)])
            p.act(Ed[:, s, :], Dm[:, s, :], AF.Exp, reads=[("Dm", s)], writes=[("Ed", s)])
            p.act(ER[:, s, :], ps[rb][:, :], AF.Exp, reads=[("ps", rb)], writes=[("ER", s)])
            gb = 2 + s
            p.mm(ps[gb][:, :128], BT[:, tk:tk + 128], CT[:, tk:tk + 128], True, True,
                 reads=[("BT", tile_of(tk)), ("CT", tile_of(tk))], writes=[("ps", gb)])
            p.tt("dve", Gm[:, s, :], ps[gb][:, :128], tri[:, d, :], ALU.mult, reads=[("ps", gb), "tri"], writes=[("Gm", s)])
            for e in range(4):
                p.tt("pool", LT[:, s, e * 128:(e + 1) * 128], Ed[:, s, e * 128:(e + 1) * 128], Gm[:, s, :], ALU.mult,
                     reads=[("Ed", s), ("Gm", s)], writes=[("LT", s, e)])
                p.tt("pool" if e % 2 else "dve", CE[:, s, e * 128:(e + 1) * 128], ER[:, s, e * 128:(e + 1) * 128], CT[:, tk:tk + 128], ALU.mult,
                     reads=[("ER", s), ("CT", tile_of(tk))], writes=[("CE", s, e)])
            p.tt("dve", dsc[:, s, 1, :], dt[:, c, 4 * d:4 * d + 4], sm[:, s, 1, :], ALU.mult, reads=["dt", ("sm", s, 1)], writes=[("dsc", s)])
            for e in range(4):
                p.ts("pool", xdt[:, s, e * 64:(e + 1) * 64], xs[:, c, e * 64:(e + 1) * 64], dt[:, c, 4 * d + e:4 * d + e + 1], None, ALU.mult,
                     reads=[("xs", c), "dt"], writes=[("xdt", s, e)])
                p.ts("dve", xdd[:, s, e * 64:(e + 1) * 64], xs[:, c, e * 64:(e + 1) * 64], dsc[:, s, 1, e:e + 1], None, ALU.mult,
                     reads=[("xs", c), ("dsc", s)], writes=[("xdd", s)])
            yb = 4 + s
            for e in range(4):
                p.mm(ps[yb][:, e * 64:(e + 1) * 64], LT[:, s, e * 128:(e + 1) * 128], xdt[:, s, e * 64:(e + 1) * 64], True, False,
                     reads=[("LT", s, e), ("xdt", s, e)], writes=[("ps", yb)], inc=False)
                p.mm(ps[yb][:, e * 64:(e + 1) * 64], CE[:, s, e * 128:(e + 1) * 128], Hbf[:, e * 64:(e + 1) * 64], False, True,
                     reads=[("CE", s, e), "Hbf"], writes=[("ps", yb)], inc=True)
            p.mm(ps[6][:, :256], Btok[:, c, :], xdd[:, s, :], True, True, reads=[("Btok", c), ("xdd", s)], writes=[("ps", 6)])
            for e in range(4):
                p.stt("dve", H[:, e * 64:(e + 1) * 64], H[:, e * 64:(e + 1) * 64], sm[:, s, 2, e:e + 1], ps[6][:, e * 64:(e + 1) * 64],
                      ALU.mult, ALU.add, reads=["H", ("sm", s, 2), ("ps", 6)], writes=["H"])
            p.copy("act", Hbf[:, :], H[:, :], reads=["H"], writes=["Hbf"])
            if not final_pass:
                p.copy("act", ysb[:, s, :], ps[yb][:, :256], reads=[("ps", yb)], writes=[("ysb", s)])
                p.dma("sp", Yf[tk:tk + 128, :], ysb[:, s, :], reads=[("ysb", s)], writes=[("Yf", c)])
            else:
                p.dma("sp", yfl[:, s, :], Yf[tk:tk + 128, :], reads=[("Yf", c)], writes=[("yfl", s)])
                p.dma("sp", hz[:, s, :, :], hv[:, :, tk:tk + 128], writes=[("hz", s)])
                for k in range(16):
                    p.mm(ps[7][:, :256], hz[:, s, k, :], wz[:, k, :], k == 0, k == 15, reads=[("hz", s), "wz"], writes=[("ps", 7)])
                p.act(zs[:, s, :], ps[7][:, :256], AF.Silu, reads=[("ps", 7)], writes=[("zs", s)])
                p.tt("dve", ysb[:, s, :], ps[yb][:, :256], yfl[:, s, :], ALU.add, reads=[("ps", yb), ("yfl", s)], writes=[("ysb", s)])
                p.tt("pool", yfl[:, s, :], xs[:, c, :], Drep[:, :], ALU.mult, reads=[("xs", c), "Drep"], writes=[("yfl", s)])
                p.tt("pool", ysb[:, s, :], ysb[:, s, :], yfl[:, s, :], ALU.add, reads=[("ysb", s), ("yfl", s)], writes=[("ysb", s)])
                p.tt("pool", ysb[:, s, :], ysb[:, s, :], zs[:, s, :], ALU.mult, reads=[("ysb", s), ("zs", s)], writes=[("ysb", s)])
                p.dma("sp", yz[tk:tk + 128, :], ysb[:, s, :], reads=[("ysb", s)], is_out=True)

        def tile_of(tk):
            return 0 if tk < 256 else 1 + (tk - 256) // 512

        order_f = [0, 1] + list(range(2, 34))
        order_b = [1, 0] + list(range(33, 1, -1))
        for n, c in enumerate(order_f):
            chunk_step(c, 0, n == 0, False)
        for n, c in enumerate(order_b):
            chunk_step(c, 1, n == 0, True)
        p.finish()
    return nc


def chunkvec(v):
    v = np.asarray(v, np.float32)
    lead = v.shape[:-1]
    return np.ascontiguousarray(np.moveaxis(v.reshape(lead + (16, 128)), -1, 0))


def rope_tables():
    quarter = 32
    inv = (10000.0 ** (-np.arange(quarter, dtype=np.float32) / quarter)).astype(np.float32)
    t = np.arange(4096)
    row = (t // 64).astype(np.float32)
    col = (t % 64).astype(np.float32)
    cosT = np.zeros((128, 4096), np.float32)
    sinT = np.zeros((128, 4096), np.float32)
    for d in range(128):
        pos = row if d < 64 else col
        ang = (pos * inv[d % 32]).astype(np.float32)
        cosT[d] = np.cos(ang)
        sinT[d] = np.sin(ang)
    prot = np.zeros((128, 128), np.float32)
    for m in range(128):
        if (m % 64) < 32:
            prot[m + 32, m] = -1.0
        else:
            prot[m - 32, m] = 1.0
    return cosT, sinT, prot


def na_masks():
    def one(i, s):
        j = 2 * i + s - 2
        m = np.zeros((2, 64, 4, 64), np.float32)
        if j < 0 or j > 31:
            return m.reshape(128, 256)
        kr = 2 * j + np.arange(2)[:, None, None, None]
        kc = np.arange(64)[None, :, None, None]
        qr = 4 * i + np.arange(4)[None, None, :, None]
        qc = np.arange(64)[None, None, None, :]
        r0 = np.clip(qr - 4, 0, 56)
        c0 = np.clip(qc - 8, 0, 48)
        ok = (kr >= r0) & (kr <= r0 + 7) & (kc >= c0) & (kc < c0 + 16)
        return ok.astype(np.float32).reshape(128, 256)
    reps = {0: [0], 1: list(range(1, 15)), 2: [15]}
    out = np.zeros((128, 3, 6, 256), np.float32)
    for cls, iis in reps.items():
        for s in range(6):
            ms = [one(i, s) for i in iis if 0 <= 2 * i + s - 2 <= 31]
            if not ms:
                continue
            for m in ms[1:]:
                assert np.array_equal(m, ms[0])
            out[:, cls, s, :] = ms[0]
    return out.astype(bf)


def na_bias(rpb_h2):
    krl = np.arange(2)[:, None, None, None]
    kc = np.arange(64)[None, :, None, None]
    qrl = np.arange(4)[None, None, :, None]
    qc = np.arange(64)[None, None, None, :]
    dc = np.clip(kc - qc, -15, 15) + 15
    out = np.zeros((128, 2, 6, 256), np.float32)
    for s in range(6):
        delta = 2 * s - 4
        dr = np.clip(delta + krl - qrl + 7, 0, 14)
        drb, dcb = np.broadcast_arrays(dr, dc)
        for e in range(2):
            out[:, e, s, :] = rpb_h2[e][drb, dcb].reshape(128, 256)
    return out


def ssd_consts():
    k = np.arange(128)[:, None]
    i = np.arange(128)[None, :]
    tri = np.stack([(k <= i), (k >= i)], 1).astype(np.float32)
    ident = np.eye(128, dtype=np.float32)
    return np.ascontiguousarray(tri), ident


def ssd_inputs(q, w_in, conv_w, conv_b, dt_bias, a_log, ssd_d):
    g = q // 2
    xo = 4096 + q * 256
    Bo = 4096 + 1024 + g * 128
    Co = 4096 + 1024 + 256 + g * 128
    wx = w_in[:, xo:xo + 256]
    wbc = np.concatenate([w_in[:, Bo:Bo + 128], w_in[:, Co:Co + 128]], 1)
    wz = w_in[:, 3072 + q * 256:3072 + (q + 1) * 256]
    wdt = np.concatenate([w_in[:, 5632 + q * 4:5632 + q * 4 + 4], w_in[:, 5632 + 16 + q * 4:5632 + 16 + q * 4 + 4]], 1)
    cidx = [np.arange(q * 256, q * 256 + 128), np.arange(q * 256 + 128, q * 256 + 256),
            np.arange(1024 + g * 128, 1024 + (g + 1) * 128), np.arange(1024 + 256 + g * 128, 1024 + 256 + (g + 1) * 128)]
    convw = np.stack([conv_w[:, ix].T for ix in cidx], 1)
    convb = np.stack([conv_b[ix] for ix in cidx], 1)
    dtb8 = np.concatenate([dt_bias[0, q * 4:q * 4 + 4], dt_bias[1, q * 4:q * 4 + 4]])
    al8 = np.concatenate([a_log[0, q * 4:q * 4 + 4], a_log[1, q * 4:q * 4 + 4]])
    dtb = np.broadcast_to(dtb8, (128, 34, 8))
    alog = np.broadcast_to(al8, (128, 34, 8))
    Drep = np.broadcast_to(np.repeat(ssd_d[q * 4:q * 4 + 4], 64), (128, 256))
    c = lambda a: np.ascontiguousarray(a, dtype=np.float32)
    return dict(wx=c(wx), wbc=c(wbc), wz=c(wz), wdt=c(wdt), convw=c(convw), convb=c(convb), dtb=c(dtb), alog=c(alog), Drep=c(Drep))


_CACHE = {}


def _prog(name, fn):
    if name not in _CACHE:
        _CACHE[name] = fn()
    return _CACHE[name]


def dft_consts():
    if "dft" in _CACHE:
        return _CACHE["dft"]
    L, CG, LC = 4096, 512, 256

    def cs(n):
        idx = (np.outer(np.arange(n), np.arange(n)) % n).astype(np.float64)
        a = 2 * np.pi * idx / n
        return np.cos(a), np.sin(a)
    c, s = cs(CG)
    CC = c.reshape(4, 128, 512).transpose(1, 0, 2).astype(bf)
    SC = s.reshape(4, 128, 512).transpose(1, 0, 2).astype(bf)
    c, s = cs(L)
    CLt = c.reshape(32, 128, 16, 256).transpose(2, 1, 0, 3).astype(bf)
    SLt = s.reshape(32, 128, 16, 256).transpose(2, 1, 0, 3).astype(bf)
    c, s = cs(LC)
    CLc = c.reshape(2, 128, 256).transpose(1, 0, 2).astype(bf)
    SLc = s.reshape(2, 128, 256).transpose(1, 0, 2).astype(bf)
    ca = np.ascontiguousarray
    _CACHE["dft"] = dict(CC=ca(CC), SC=ca(SC), CLt=ca(CLt), SLt=ca(SLt), CLc=ca(CLc), SLc=ca(SLc))
    return _CACHE["dft"]


def _run(nc, ins):
    res = run_bass_kernel_spmd(nc, ins, core_ids=list(range(8)))
    return res.results


def kernel(x, c, ctx, c_ctx, mod_w, mod_b, norm_g, ffn_w1, ffn_w3, ffn_w2, mix_w_in, mix_w_out, qk_g, rpb,
           conv_w, conv_b, dt_bias, a_log, ssd_d, ssd_norm_g, fourier_w):
    f32 = np.float32
    A = lambda a: np.asarray(a, dtype=f32)
    ca = np.ascontiguousarray
    x, c, ctx, c_ctx = A(x), A(c), A(ctx), A(c_ctx)
    mod_w, mod_b, norm_g = A(mod_w), A(mod_b), A(norm_g)
    ffn_w1, ffn_w3, ffn_w2 = A(ffn_w1), A(ffn_w3), A(ffn_w2)
    mix_w_in, mix_w_out, qk_g, rpb = A(mix_w_in), A(mix_w_out), A(qk_g), A(rpb)
    conv_w, conv_b, dt_bias, a_log, ssd_d, ssd_norm_g, fourier_w = A(conv_w), A(conv_b), A(dt_bias), A(a_log), A(ssd_d), A(ssd_norm_g), A(fourier_w)

    c3 = np.stack([c[0], c[1], c_ctx], 0)
    cT = ca(c3.reshape(3, 16, 128).transpose(2, 1, 0))
    ins = []
    for k in range(8):
        cols = slice(k * 2304, (k + 1) * 2304)
        ins.append({"cT": cT, "mw": ca(mod_w[:, :, cols]), "mb": ca(mod_b[:, cols].reshape(4, 18, 128).transpose(2, 0, 1))})
    r = _run(_prog("mod", build_mod), ins)
    m = np.zeros((4, 3, 18432), f32)
    for k in range(8):
        m[:, :, k * 2304:(k + 1) * 2304] = r[k]["mo"].transpose(1, 3, 2, 0).reshape(4, 3, 2304)
    m9 = m.reshape(4, 3, 9, 2048)

    xT = []
    for k in range(8):
        b, tq = k // 4, k % 4
        xT.append(ca(np.concatenate([x[b, tq * 1024:(tq + 1) * 1024], ctx[b, tq * 64:(tq + 1) * 64]], 0).T))
    cosT, sinT, prot = rope_tables()
    mask01 = na_masks()
    tri, ident = ssd_consts()
    zeros2048 = np.zeros(2048, f32)

    for i in range(4):
        j = i // 2
        ins = []
        for k in range(8):
            b = k // 4
            vec = np.stack([m9[i, b, 0], m9[i, b, 1], m9[i, b, 2], m9[i, 2, 0], m9[i, 2, 1], m9[i, 2, 2], norm_g[i, 0],
                            m9[i, b, 3], m9[i, b, 4], m9[i, 2, 3], m9[i, 2, 4], norm_g[i, 1]], 0)
            ins.append({"xT": xT[k], "vecd": chunkvec(vec), "w1": ffn_w1[i, 0], "w3": ffn_w3[i, 0], "w2": ffn_w2[i, 0]})
        r = _run(_prog("A", build_A), ins)
        xT = [r[k]["xo"] for k in range(8)]
        hT = [np.concatenate([np.concatenate([r[b * 4 + tq]["ho"][:, 1024:] for tq in range(4)], 1),
                              np.concatenate([r[b * 4 + tq]["ho"][:, :1024] for tq in range(4)], 1)], 1) for b in range(2)]
        if i % 2 == 0:
            w_in = mix_w_in[j]
            ins = []
            for k in range(8):
                b, q = k // 4, k % 4
                cs_ = slice(q * 256, (q + 1) * 256)
                ins.append({"hT": ca(hT[b]), "wq": ca(w_in[:, 0:1024][:, cs_]), "wk": ca(w_in[:, 1024:2048][:, cs_]),
                            "wv": ca(w_in[:, 2048:3072][:, cs_]), "qkg": ca(qk_g[j].T), "cosT": cosT, "sinT": sinT, "prot": prot,
                            "biasblk": na_bias(rpb[j][2 * q:2 * q + 2]), "mask01": mask01})
            rn = _run(_prog("na", build_na), ins)
            ins = []
            for k in range(8):
                b, q = k // 4, k % 4
                d = ssd_inputs(q, w_in, conv_w[j], conv_b[j], dt_bias[j], a_log[j], ssd_d[j])
                d["hT"] = ca(hT[b])
                d["tri"] = tri
                d["ident"] = ident
                ins.append(d)
            rs = _run(_prog("ssd", build_ssd), ins)
            mixT = []
            for k in range(8):
                b, tq = k // 4, k % 4
                rows = []
                for q in range(4):
                    a_ = rn[b * 4 + q]["attnT"]
                    rows.append(np.concatenate([a_[:, 256 + tq * 1024:256 + (tq + 1) * 1024], a_[:, tq * 64:(tq + 1) * 64]], 1))
                for q in range(4):
                    y_ = rs[b * 4 + q]["yz"]
                    rows.append(np.concatenate([y_[256 + tq * 1024:256 + (tq + 1) * 1024], y_[tq * 64:(tq + 1) * 64]], 0).T)
                mixT.append(ca(np.concatenate(rows, 0)))
            W = mix_w_out[j]
            g9 = np.concatenate([np.zeros(1024, f32), ssd_norm_g[j]])
            cname, cfn = "Ce", (lambda: build_C(True))
        else:
            K = dft_consts()
            ins = []
            for k in range(8):
                b, grp = k // 4, k % 4
                d = dict(K)
                d["hg"] = ca(hT[b][grp * 512:(grp + 1) * 512, 256:])
                d["hcg"] = ca(hT[b][grp * 512:(grp + 1) * 512, :256])
                ins.append(d)
            rf = _run(_prog("fourier", build_fourier), ins)
            mixT = []
            for k in range(8):
                b, tq = k // 4, k % 4
                rows = []
                for grp in range(4):
                    o = rf[b * 4 + grp]
                    rows.append(np.concatenate([o["FT"][:, tq * 1024:(tq + 1) * 1024], o["FcT"][:, tq * 64:(tq + 1) * 64]], 1))
                mixT.append(ca(np.concatenate(rows, 0)))
            W = fourier_w[j]
            g9 = zeros2048
            cname, cfn = "Co", (lambda: build_C(False))
        ins = []
        for k in range(8):
            b = k // 4
            vec = np.stack([m9[i, b, 6], m9[i, b, 7], m9[i, b, 8], m9[i, 2, 6], m9[i, 2, 7], m9[i, 2, 8], norm_g[i, 2],
                            m9[i, b, 5], m9[i, 2, 5], g9], 0)
            ins.append({"xT": xT[k], "mixT": mixT[k], "vecd": chunkvec(vec), "W": W,
                        "w1": ffn_w1[i, 1], "w3": ffn_w3[i, 1], "w2": ffn_w2[i, 1]})
        r = _run(_prog(cname, cfn), ins)
        xT = [r[k]["xo"] for k in range(8)]

    out = np.zeros((2, 4096, 2048), f32)
    for k in range(8):
        b, tq = k // 4, k % 4
        out[b, tq * 1024:(tq + 1) * 1024] = xT[k][:, :1024].T
    return out
```
